# Optimizing a Trainium2 kernel written in Bass

```python
import math
import jax, jax.numpy as jnp
from jax import lax
import numpy as np

D_MODEL = 1024
BATCH = 16
SEQ = 2048
DEPTH = 1

MIX_WIDTH = D_MODEL
FNET_WIDTH = MIX_WIDTH // 2
FNET_GROUPS = 4
FNET_GROUP_DIM = FNET_WIDTH // FNET_GROUPS
HEAD_DIM = 64
N_Q_HEADS = (MIX_WIDTH - FNET_WIDTH) // HEAD_DIM
N_KV_HEADS = 2
GQA_GROUP = N_Q_HEADS // N_KV_HEADS
ATTN_WIDTH = N_Q_HEADS * HEAD_DIM
KV_WIDTH = N_KV_HEADS * HEAD_DIM
IN_WIDTH = FNET_WIDTH + ATTN_WIDTH + 2 * KV_WIDTH
D_FF = 2752
GRID_W = 64
AXIS_DIM = HEAD_DIM // 2
ROPE_THETA = 10000.0
Q_BLOCK = 128
EPS = 1e-6

kernel_name = "hybrid_fnet_gqa_axial_macaron_encoder"


def rms_norm(x, g):
    xf = x.astype(jnp.float32)
    y = xf * lax.rsqrt(jnp.mean(xf * xf, axis=-1, keepdims=True) + EPS)
    return (y * g.astype(jnp.float32)).astype(x.dtype)


def swiglu(h, w_gate, w_up, w_down):
    return (jax.nn.silu(h @ w_gate) * (h @ w_up)) @ w_down


def axial_rope_tables(seq):
    rows = seq // GRID_W
    row_idx = jnp.repeat(jnp.arange(rows, dtype=jnp.float32), GRID_W)
    col_idx = jnp.tile(jnp.arange(GRID_W, dtype=jnp.float32), rows)
    inv_freq = ROPE_THETA ** (-jnp.arange(0, AXIS_DIM, 2, dtype=jnp.float32) / AXIS_DIM)
    ang_r = row_idx[:, None] * inv_freq[None, :]
    ang_c = col_idx[:, None] * inv_freq[None, :]
    return jnp.cos(ang_r), jnp.sin(ang_r), jnp.cos(ang_c), jnp.sin(ang_c)


def rotate(x, cos, sin):
    half = x.shape[-1] // 2
    x1, x2 = x[..., :half], x[..., half:]
    c = cos[None, :, None, :].astype(x.dtype)
    s = sin[None, :, None, :].astype(x.dtype)
    return jnp.concatenate([x1 * c - x2 * s, x1 * s + x2 * c], axis=-1)


def apply_axial_rope(x, tables):
    cr, sr, cc, sc = tables
    return jnp.concatenate([rotate(x[..., :AXIS_DIM], cr, sr),
                            rotate(x[..., AXIS_DIM:], cc, sc)], axis=-1)


def fourier_mixer(hf, fnet_w, fnet_b):
    b, s, _ = hf.shape
    g = hf.reshape(b, s, FNET_GROUPS, FNET_GROUP_DIM).astype(jnp.float32)
    mixed = jnp.real(jnp.fft.fft2(g, axes=(1, 3), norm="ortho")).astype(hf.dtype)
    out = jnp.einsum('bsgc,gcd->bsgd', mixed, fnet_w) + fnet_b[None, None]
    return out.reshape(b, s, FNET_WIDTH)


def gqa_attention(q, k, v, q_norm, k_norm):
    b, s, _ = q.shape
    q = rms_norm(q.reshape(b, s, N_Q_HEADS, HEAD_DIM), q_norm)
    k = rms_norm(k.reshape(b, s, N_KV_HEADS, HEAD_DIM), k_norm)
    v = v.reshape(b, s, N_KV_HEADS, HEAD_DIM)
    tables = axial_rope_tables(s)
    q = apply_axial_rope(q, tables)
    k = apply_axial_rope(k, tables)
    scale = 1.0 / math.sqrt(HEAD_DIM)
    n_blocks = s // Q_BLOCK
    qb = q.reshape(b, n_blocks, Q_BLOCK, N_KV_HEADS, GQA_GROUP, HEAD_DIM)
    qb = jnp.moveaxis(qb, 1, 0)

    def attend_block(q_blk):
        scores = jnp.einsum('bqhgd,bkhd->bhgqk', q_blk, k).astype(jnp.float32) * scale
        p = jax.nn.softmax(scores, axis=-1).astype(v.dtype)
        return jnp.einsum('bhgqk,bkhd->bqhgd', p, v)

    o = lax.map(attend_block, qb)
    o = jnp.moveaxis(o, 0, 1)
    return o.reshape(b, s, ATTN_WIDTH)


def setup_inputs(seed: int = 0) -> dict:
    key = jax.random.key(seed)
    ks = jax.random.split(key, 20)
    D, F = D_MODEL, D_FF
    nrm = lambda k, shape, fan_in: jax.random.normal(k, shape, jnp.float32) * fan_in ** -0.5
    gain = lambda k, n: 1.0 + 0.05 * jax.random.normal(k, (n,), jnp.float32)
    return {
        "x": jax.random.normal(ks[0], (BATCH, SEQ, D), jnp.float32),
        "ffn1_norm": gain(ks[1], D),
        "ffn1_w_gate": nrm(ks[2], (D, F), D),
        "ffn1_w_up": nrm(ks[3], (D, F), D),
        "ffn1_w_down": nrm(ks[4], (F, D), F),
        "mix_norm": gain(ks[5], D),
        "w_in": nrm(ks[6], (D, IN_WIDTH), D),
        "fnet_w": nrm(ks[7], (FNET_GROUPS, FNET_GROUP_DIM, FNET_GROUP_DIM), FNET_GROUP_DIM),
        "fnet_b": 0.02 * jax.random.normal(ks[8], (FNET_GROUPS, FNET_GROUP_DIM), jnp.float32),
        "q_norm": gain(ks[9], HEAD_DIM),
        "k_norm": gain(ks[10], HEAD_DIM),
        "w_out": nrm(ks[11], (MIX_WIDTH, D), MIX_WIDTH),
        "ffn2_norm": gain(ks[12], D),
        "ffn2_w_gate": nrm(ks[13], (D, F), D),
        "ffn2_w_up": nrm(ks[14], (D, F), D),
        "ffn2_w_down": nrm(ks[15], (F, D), F),
        "final_norm": gain(ks[16], D),
    }


def reference(x, ffn1_norm, ffn1_w_gate, ffn1_w_up, ffn1_w_down, mix_norm, w_in,
              fnet_w, fnet_b, q_norm, k_norm, w_out, ffn2_norm, ffn2_w_gate,
              ffn2_w_up, ffn2_w_down, final_norm):
    for _ in range(DEPTH):
        x = x + 0.5 * swiglu(rms_norm(x, ffn1_norm), ffn1_w_gate, ffn1_w_up, ffn1_w_down)
        h = rms_norm(x, mix_norm)
        u = h @ w_in
        o0 = FNET_WIDTH
        o1 = o0 + ATTN_WIDTH
        o2 = o1 + KV_WIDTH
        f_out = fourier_mixer(u[..., :o0], fnet_w, fnet_b)
        a_out = gqa_attention(u[..., o0:o1], u[..., o1:o2], u[..., o2:], q_norm, k_norm)
        x = x + jnp.concatenate([f_out, a_out], axis=-1) @ w_out
        x = x + 0.5 * swiglu(rms_norm(x, ffn2_norm), ffn2_w_gate, ffn2_w_up, ffn2_w_down)
    return rms_norm(x, final_norm)
```

```python
import math, os
DBG = os.environ.get('DBG', '')
from bisect import bisect_left
from contextlib import ExitStack

import numpy as np
import ml_dtypes
import concourse.bass as bass
import concourse.mybir as mybir
from concourse.bass_utils import run_bass_kernel_spmd

F32 = mybir.dt.float32
BF16 = mybir.dt.bfloat16
AF = mybir.ActivationFunctionType
ALU = mybir.AluOpType
AX = mybir.AxisListType

D = 1024
S = 2048
DFF = 2752
NSEQ = 2
NT = S // 128
EPS = 1e-6
N_CORES = 8

ENGS = ['pe', 'act', 'dve', 'pool', 'sp']
N_DMA_SEMS = 24


class Ticket:
    __slots__ = ('eng', 'idx', 'sem', 'val')

    def __init__(self, eng=None, idx=None, sem=None, val=None):
        self.eng = eng; self.idx = idx; self.sem = sem; self.val = val


class Res:
    __slots__ = ('w', 'r_eng', 'r_dma', 'excl')

    def __init__(self, excl=False):
        self.w = None
        self.r_eng = {}
        self.r_dma = []
        self.excl = excl


class Prog:
    def __init__(self, nc):
        self.nc = nc
        self.ops = {e: [] for e in ENGS}
        self.flag_idx = {e: [] for e in ENGS}
        self.flag_val = {e: [] for e in ENGS}
        self.seen = {e: {} for e in ENGS}
        self.dma_next = {'sp': 0, 'pool': 0}
        self.dma_val = {'sp': [0] * N_DMA_SEMS, 'pool': [0] * N_DMA_SEMS}
        self.out_tickets = []

    def resolve(self, t):
        if t.sem is not None:
            return (t.sem, t.val)
        e = t.eng
        fi = self.flag_idx[e]
        if fi and fi[-1] >= t.idx:
            k = bisect_left(fi, t.idx)
            return (('eng', e), self.flag_val[e][k])
        v = len(fi) + 1
        fi.append(t.idx)
        self.flag_val[e].append(v)
        self.ops[e][t.idx]['inc'] = (('eng', e), 1)
        return (('eng', e), v)

    def emit(self, eng, fn, reads=(), writes=(), dma=False, is_out=False):
        rd = []
        wr = list(writes)
        for r in reads:
            (wr if r.excl else rd).append(r)
        deps = []
        for r in rd:
            if r.w is not None:
                deps.append((r.w, True))
        for w in wr:
            if w.w is not None:
                deps.append((w.w, w.excl))
            for t in w.r_eng.values():
                deps.append((t, False))
            for t in w.r_dma:
                deps.append((t, False))
        waits = []
        seen = self.seen[eng]
        for t, raw in deps:
            if not dma and t.eng == eng and eng == 'pe':
                continue
            k, v = self.resolve(t)
            if seen.get(k, 0) >= v:
                continue
            seen[k] = v
            waits.append((k, v))
        idx = len(self.ops[eng])
        op = dict(fn=fn, waits=waits, inc=None)
        if dma:
            s = self.dma_next[eng]
            self.dma_next[eng] = (s + 1) % N_DMA_SEMS
            key = ('dma_' + eng, s)
            prev = self.dma_val[eng][s]
            if prev > 0 and seen.get(key, 0) < prev:
                seen[key] = prev
                waits.append((key, prev))
            self.dma_val[eng][s] = prev + 16
            op['inc'] = (key, 16)
            tk = Ticket(sem=key, val=prev + 16)
            if is_out:
                self.out_tickets.append(tk)
        else:
            tk = Ticket(eng=eng, idx=idx)
        self.ops[eng].append(op)
        for r in rd:
            if dma:
                r.r_dma.append(tk)
            else:
                r.r_eng[eng] = tk
        for w in wr:
            w.w = tk
            w.r_eng = {}
            w.r_dma = []
        return tk

    def finish(self, eng='sp'):
        waits = []
        seen = self.seen[eng]
        for t in self.out_tickets:
            k, v = self.resolve(t)
            if seen.get(k, 0) >= v:
                continue
            seen[k] = v
            waits.append((k, v))
        self.ops[eng].append(dict(fn=None, waits=waits, inc=None))

    def replay(self, st):
        nc = self.nc
        sems = {}
        for e in ENGS:
            sems[('eng', e)] = st.enter_context(nc.semaphore('p_' + e))
        for q in ('sp', 'pool'):
            for s in range(N_DMA_SEMS):
                sems[('dma_' + q, s)] = st.enter_context(nc.semaphore('d%s%d' % (q, s)))
        block = st.enter_context(nc.Block())

        def mk(name):
            ops = self.ops[name]

            def body(engine):
                for op in ops:
                    for (k, v) in op['waits']:
                        engine.wait_ge(sems[k], v)
                    if op['fn'] is None:
                        continue
                    ins = op['fn'](engine)
                    if op['inc'] is not None:
                        ins.then_inc(sems[op['inc'][0]], op['inc'][1])
            return body

        block.tensor(mk('pe'))
        block.scalar(mk('act'))
        block.vector(mk('dve'))
        block.gpsimd(mk('pool'))
        block.sync(mk('sp'))


KB = 1024
ARENA_KB = 86


def build_program(stop=None):
    nc = bass.Bass("TRN2", target_bir_lowering=False)
    dt_in = lambda name, shape, dt=F32: nc.dram_tensor(name, shape, dt, kind="ExternalInput").ap()
    x_d = dt_in("x", [NSEQ * S, D])
    y_d = nc.dram_tensor("y", [NSEQ * S, D], F32, kind="ExternalOutput").ap()
    g_d = {n: dt_in(n, [1, D]) for n in ("ffn1_norm", "mix_norm", "ffn2_norm", "final_norm")}
    wgate_d = {1: dt_in("ffn1_w_gate", [D, DFF]), 2: dt_in("ffn2_w_gate", [D, DFF])}
    wup_d = {1: dt_in("ffn1_w_up", [D, DFF]), 2: dt_in("ffn2_w_up", [D, DFF])}
    wdown_d = {1: dt_in("ffn1_w_down", [DFF, D]), 2: dt_in("ffn2_w_down", [DFF, D])}
    win_d = dt_in("w_in", [D, 1280])
    wout_d = dt_in("w_out", [D, D])
    fnetw_d = dt_in("fnet_w", [4, 128, 128])
    fnetb_d = dt_in("fnet_bT", [128, 4])
    gqk_d = dt_in("gqk", [1, 640])
    ident_d = dt_in("ident", [128, 128], BF16)
    ccsc_d = dt_in("ccsc", [128, 256], BF16)
    csmat_d = dt_in("csmat", [16, 128, 2, S], BF16)
    cos_d = dt_in("cos_t", [128, NT * 64])
    sin_d = dt_in("sin_t", [128, NT * 64])

    P = Prog(nc)
    with ExitStack() as st:
        sb = lambda name, shape, dt: st.enter_context(nc.sbuf_tensor(name, shape, dt))
        x = sb("x_sb", [128, NT, D], F32)
        R_x = [[Res(), Res()] for _ in range(NT)]
        hm = sb("hm", [128, 8, S], BF16)
        R_hm = [[Res() for _ in range(NT)] for _ in range(8)]
        arena = sb("arena", [128, ARENA_KB * KB // 2], BF16)
        R_pg = [Res() for _ in range(ARENA_KB)]

        def aview(off_b, size_b, dt=BF16):
            ap = arena[:, off_b // 2:(off_b + size_b) // 2]
            if dt == F32:
                ap = ap.bitcast(F32)
            p0 = off_b // KB
            p1 = (off_b + size_b + KB - 1) // KB
            return ap, R_pg[p0:p1]

        gbuf = [sb("gbuf%d" % i, [128, D], F32) for i in range(2)]
        R_g = [Res() for _ in range(2)]
        gqk = sb("gqk_sb", [128, 640], F32); R_gqk = Res()
        ident = sb("ident_sb", [128, 128], BF16); R_ident = Res()
        ccsc = sb("ccsc_sb", [128, 256], BF16); R_ccsc = Res()
        fnetw = sb("fnetw_sb", [128, 4, 128], BF16); R_fnetw = Res()
        fnetb = sb("fnetb_sb", [128, 4], F32); R_fnetb = Res()
        hbuf = [sb("hbuf%d" % i, [128, D], BF16) for i in range(2)]
        R_h = [Res() for _ in range(2)]
        junk = sb("junk", [128, D], BF16); R_junk = Res()
        ssq = sb("ssq", [128, NT], F32); R_ssq = [Res() for _ in range(NT)]
        sv = sb("sv", [128, NT], F32); R_sv = Res()
        slog = sb("slog", [128, NT], F32); R_slog = Res()
        rstd = sb("rstd", [128, NT], F32); R_rstd = Res()
        qss = [sb("qss%d" % i, [128, 16], F32) for i in range(2)]; R_qss = [Res() for _ in range(2)]
        qv = [sb("qv%d" % i, [128, 16], F32) for i in range(2)]; R_qv = [Res() for _ in range(2)]
        ql = [sb("ql%d" % i, [128, 16], F32) for i in range(2)]; R_ql = [Res() for _ in range(2)]
        qr = [sb("qr%d" % i, [128, 16], F32) for i in range(2)]; R_qr = [Res() for _ in range(2)]
        qrot = [sb("qrot%d" % i, [128, 512], BF16) for i in range(2)]; R_qrot = [Res() for _ in range(2)]
        krot = [sb("krot%d" % i, [128, 2, 2, 64], BF16) for i in range(2)]; R_krot = [Res() for _ in range(2)]

        pp = [st.enter_context(nc.psum_tensor("pp%d" % i, [128, 1024], F32)) for i in range(4)]
        R_bank = [Res(excl=True) for _ in range(8)]

        def bank(k, rows=128, cols=512):
            return pp[k // 2][0:rows, (k % 2) * 512:(k % 2) * 512 + cols]

        rot = [0]

        def next_bank(cands):
            b = cands[rot[0] % len(cands)]
            rot[0] += 1
            return b

        evac_rot = [0]

        def evac(out_ap, in_ap, reads, writes, bias=None):
            evac_rot[0] += 1
            if bias is not None or evac_rot[0] % 2 == 0:
                if bias is not None:
                    P.emit('act', lambda e: e.activation(out=out_ap, in_=in_ap, func=AF.Identity, bias=bias),
                           reads=reads, writes=writes)
                else:
                    P.emit('act', lambda e: e.activation(out=out_ap, in_=in_ap, func=AF.Copy),
                           reads=reads, writes=writes)
            else:
                P.emit('dve', lambda e: e.tensor_copy(out=out_ap, in_=in_ap), reads=reads, writes=writes)

        P.emit('sp', lambda e: e.dma_start(out=ident[:], in_=ident_d), writes=[R_ident], dma=True)
        P.emit('sp', lambda e: e.dma_start(out=ccsc[:], in_=ccsc_d), writes=[R_ccsc], dma=True)
        P.emit('sp', lambda e: e.dma_start(out=fnetb[:], in_=fnetb_d), writes=[R_fnetb], dma=True)
        P.emit('sp', lambda e: e.dma_start(out=gqk[:], in_=gqk_d.partition_broadcast(128)), writes=[R_gqk], dma=True)
        P.emit('pool', lambda e: e.dma_start(out=fnetw[:], in_=fnetw_d.rearrange("g c d -> c g d")),
               writes=[R_fnetw], dma=True)

        gsel = [0]

        def load_gain(name):
            i = gsel[0] % 2
            gsel[0] += 1
            P.emit('sp', lambda e: e.dma_start(out=gbuf[i][:], in_=g_d[name].partition_broadcast(128)),
                   writes=[R_g[i]], dma=True)
            return i

        def rms_stats():
            for T in range(NT):
                P.emit('act', lambda e, T=T: e.activation(out=junk[:], in_=x[:, T, :], func=AF.Square,
                                                          accum_out=ssq[:, T:T + 1]),
                       reads=R_x[T], writes=[R_junk, R_ssq[T]])
            P.emit('dve', lambda e: e.tensor_scalar(out=sv[:], in0=ssq[:], scalar1=1.0 / D, scalar2=EPS,
                                                    op0=ALU.mult, op1=ALU.add), reads=R_ssq, writes=[R_sv])
            P.emit('act', lambda e: e.activation(out=slog[:], in_=sv[:], func=AF.Ln), reads=[R_sv], writes=[R_slog])
            P.emit('act', lambda e: e.activation(out=rstd[:], in_=slog[:], func=AF.Exp, scale=-0.5),
                   reads=[R_slog], writes=[R_rstd])

        R_svb = [Res() for _ in range(4)]
        R_slogb = [Res() for _ in range(4)]
        R_rstdb = [Res() for _ in range(4)]
        svb = sb("svb", [128, NT], F32)
        slogb = sb("slogb", [128, NT], F32)
        rstdb = sb("rstdb", [128, NT], F32)

        def norm_to_hT(gi):
            for tb in range(4):
                tiles = range(tb * 4, tb * 4 + 4)
                cs = slice(tb * 4, tb * 4 + 4)
                for T in tiles:
                    P.emit('act', lambda e, T=T: e.activation(out=junk[:], in_=x[:, T, :], func=AF.Square,
                                                              accum_out=ssq[:, T:T + 1]),
                           reads=R_x[T], writes=[R_junk, R_ssq[T]])
                P.emit('dve', lambda e, cs=cs: e.tensor_scalar(out=svb[:, cs], in0=ssq[:, cs], scalar1=1.0 / D, scalar2=EPS,
                                                               op0=ALU.mult, op1=ALU.add),
                       reads=[R_ssq[T] for T in tiles], writes=[R_svb[tb]])
                P.emit('act', lambda e, cs=cs: e.activation(out=slogb[:, cs], in_=svb[:, cs], func=AF.Ln),
                       reads=[R_svb[tb]], writes=[R_slogb[tb]])
                P.emit('act', lambda e, cs=cs: e.activation(out=rstdb[:, cs], in_=slogb[:, cs], func=AF.Exp, scale=-0.5),
                       reads=[R_slogb[tb]], writes=[R_rstdb[tb]])
                for T in tiles:
                    j = T % 2
                    P.emit('dve', lambda e, T=T, j=j: e.scalar_tensor_tensor(
                        out=hbuf[j][:], in0=x[:, T, :], scalar=rstdb[:, T:T + 1], in1=gbuf[gi][:],
                        op0=ALU.mult, op1=ALU.mult), reads=R_x[T] + [R_rstdb[tb], R_g[gi]], writes=[R_h[j]])
                    b = next_bank([4, 5, 6, 7])
                    bv = bank(b).bitcast(BF16).rearrange("p (c t) -> p c t", c=8)
                    for c in range(8):
                        P.emit('pe', lambda e, c=c, j=j, bv=bv: e.transpose(
                            out=bv[:, c, :], in_=hbuf[j][:, c * 128:(c + 1) * 128], identity=ident[:]),
                            reads=[R_h[j], R_ident], writes=[R_bank[b]])
                    evac(hm[:, :, T * 128:(T + 1) * 128], bv, [R_bank[b]], [R_hm[c][T] for c in range(8)])

        chunks = [(i * 128, 128) for i in range(21)] + [(21 * 128, 64)]
        groups = [chunks[i:i + 4] for i in range(0, 22, 4)]

        def wbuf_views(b):
            base = b * 24 * KB
            wg, r0 = aview(base, 8 * KB)
            wu, r1 = aview(base + 8 * KB, 8 * KB)
            wd, r2 = aview(base + 16 * KB, 8 * KB)
            return (wg.rearrange("p (c f) -> p c f", c=8), r0, wu.rearrange("p (c f) -> p c f", c=8), r1,
                    wd.rearrange("p (c f) -> p c f", c=4), r2)

        def load_group(which, gi):
            grp = groups[gi]
            b = gi % 2
            wg, r0, wu, r1, wd, r2 = wbuf_views(b)
            f0 = grp[0][0]
            width = sum(c[1] for c in grp)
            P.emit('pool', lambda e: e.dma_start(out=wg[:, :, 0:width],
                                                 in_=wgate_d[which][:, f0:f0 + width].rearrange("(c p) f -> p c f", p=128)),
                   writes=r0, dma=True)
            P.emit('pool', lambda e: e.dma_start(out=wu[:, :, 0:width],
                                                 in_=wup_d[which][:, f0:f0 + width].rearrange("(c p) f -> p c f", p=128)),
                   writes=r1, dma=True)
            nfull = sum(1 for c in grp if c[1] == 128)
            if nfull:
                P.emit('pool', lambda e: e.dma_start(out=wd[:, 0:nfull, :],
                                                     in_=wdown_d[which][f0:f0 + nfull * 128, :].rearrange("(c p) d -> p c d", p=128)),
                       writes=r2, dma=True)
            if nfull < len(grp):
                fo = f0 + nfull * 128
                P.emit('pool', lambda e: e.dma_start(out=wd[0:64, nfull, :], in_=wdown_d[which][fo:fo + 64, :]),
                       writes=r2, dma=True)

        actbuf = []
        for j in range(2):
            a, r = aview(48 * KB + j * 4 * KB, 4 * KB)
            actbuf.append((a.rearrange("p (c t) -> p c t", c=4), r))
        silubuf = [aview(56 * KB + j * 2 * KB, 2 * KB, F32) for j in range(2)]

        def ffn(which, pre_loaded):
            cnt = 0
            for gi, grp in enumerate(groups):
                if gi == 0 and not pre_loaded:
                    load_group(which, 0)
                if gi + 1 < len(groups):
                    load_group(which, gi + 1)
                b = gi % 2
                wg, r0, wu, r1, wd, r2 = wbuf_views(b)
                for tb in range(4):
                    ab, r_ab = actbuf[tb % 2]
                    hT_res = lambda dc: [R_hm[dc][tb * 4 + t] for t in range(4)]
                    for ci, (fo, fsz) in enumerate(grp):
                        pair = (0, 1) if cnt % 2 == 0 else (2, 3)
                        cnt += 1
                        for (bk, wv, rw) in ((pair[0], wg, r0), (pair[1], wu, r1)):
                            for dc in range(8):
                                P.emit('pe', lambda e, bk=bk, wv=wv, dc=dc, ci=ci, fsz=fsz, tb=tb: e.matmul(
                                    bank(bk, fsz), lhsT=wv[:, dc, ci * 128:ci * 128 + fsz],
                                    rhs=hm[:, dc, tb * 512:(tb + 1) * 512], start=(dc == 0), stop=(dc == 7)),
                                    reads=rw + hT_res(dc), writes=[R_bank[bk]])
                        sl, r_sl = silubuf[cnt % 2]
                        P.emit('act', lambda e, sl=sl, fsz=fsz, bk=pair[0]: e.activation(
                            out=sl[0:fsz, :], in_=bank(bk, fsz), func=AF.Silu),
                            reads=[R_bank[pair[0]]], writes=r_sl)
                        P.emit('dve', lambda e, sl=sl, fsz=fsz, bk=pair[1], ab=ab, ci=ci: e.tensor_tensor(
                            out=ab[0:fsz, ci, :], in0=bank(bk, fsz), in1=sl[0:fsz, :], op=ALU.mult),
                            reads=[R_bank[pair[1]]] + r_sl, writes=r_ab)
                    for tt in range(4):
                        T = tb * 4 + tt
                        for dh in range(2):
                            bk = next_bank([4, 5, 6, 7])
                            for ci, (fo, fsz) in enumerate(grp):
                                P.emit('pe', lambda e, bk=bk, ab=ab, ci=ci, fsz=fsz, tt=tt, dh=dh, wd=wd, last=(ci == len(grp) - 1): e.matmul(
                                    bank(bk), lhsT=ab[0:fsz, ci, tt * 128:(tt + 1) * 128],
                                    rhs=wd[0:fsz, ci, dh * 512:(dh + 1) * 512],
                                    start=(ci == 0), stop=last),
                                    reads=r_ab + r2, writes=[R_bank[bk]])
                            P.emit('dve', lambda e, bk=bk, T=T, dh=dh: e.scalar_tensor_tensor(
                                out=x[:, T, dh * 512:(dh + 1) * 512], in0=bank(bk), scalar=0.5,
                                in1=x[:, T, dh * 512:(dh + 1) * 512], op0=ALU.mult, op1=ALU.add),
                                reads=[R_bank[bk], R_x[T][dh]], writes=[R_x[T][dh]])

        win_f, r_win_f = aview(0, 8 * KB); win_f = win_f.rearrange("p (c f) -> p c f", c=8)
        wout_f, r_wout_f = aview(8 * KB, 8 * KB); wout_f = wout_f.rearrange("p (c f) -> p c f", c=4)
        AB, r_AB_all = aview(16 * KB, 32 * KB); AB = AB.rearrange("p (t g f) -> p t g f", t=NT, g=4)
        r_AB = [R_pg[16 + 2 * T:18 + 2 * T] for T in range(NT)]
        stream = []
        for k in range(4):
            a, r = aview(48 * KB + k * 4 * KB, 4 * KB)
            stream.append((a.rearrange("p (m k) -> p m k", m=2), r))
        ufT = []
        for j in range(2):
            a, r = aview(64 * KB + j * 4 * KB, 4 * KB)
            ufT.append((a.rearrange("p (g t) -> p g t", g=4), r))
        YT = [aview(64 * KB + j * 2 * KB, 2 * KB) for j in range(2)]
        foutT, r_foutT = aview(68 * KB, 8 * KB); foutT = foutT.rearrange("p (g t) -> p g t", g=4)
        win_q, r_win_q = aview(0, 12 * KB); win_q = win_q.rearrange("p (c f) -> p c f", c=8)
        cos_t, r_cos = aview(12 * KB, 4 * KB, F32); cos_t = cos_t.rearrange("p (t f) -> p t f", t=NT)
        sin_t, r_sin = aview(16 * KB, 4 * KB, F32); sin_t = sin_t.rearrange("p (t f) -> p t f", t=NT)
        qT, r_qT = aview(24 * KB, 16 * KB); qT = qT.rearrange("p (c t) -> p c t", c=4)
        kT2, r_kT2 = aview(40 * KB, 8 * KB); kT2 = kT2.rearrange("p (c t) -> p c t", c=2)
        Vaug, r_Vaug = aview(48 * KB, 12 * KB); Vaug = Vaug.rearrange("p (t k f) -> p t k f", t=NT, k=2)
        qkb = [aview(60 * KB + j * 3 * KB, 2560, F32) for j in range(2)]
        t1b = [aview(66 * KB + j * 3 * KB, 2560, F32) for j in range(2)]
        t2b = [aview(72 * KB + j * 3 * KB, 2560, F32) for j in range(2)]
        wout_a, r_wout_a = aview(78 * KB, 8 * KB); wout_a = wout_a.rearrange("p (c f) -> p c f", c=4)
        PT = [aview(60 * KB + j * 2 * KB, 2 * KB) for j in range(3)]
        rec = [aview(66 * KB + j * 2 * KB, 2 * KB, F32) for j in range(2)]
        ystage = [aview(48 * KB + j * 4 * KB, 4 * KB, F32) for j in range(2)]

        def x_add(bk, T, dh):
            P.emit('dve', lambda e: e.tensor_tensor(out=x[:, T, dh * 512:(dh + 1) * 512], in0=bank(bk),
                                                    in1=x[:, T, dh * 512:(dh + 1) * 512], op=ALU.add),
                   reads=[R_bank[bk], R_x[T][dh]], writes=[R_x[T][dh]])

        def mix_p1_p2():
            P.emit('pool', lambda e: e.dma_start(out=win_f, in_=win_d[:, 0:512].rearrange("(c p) f -> p c f", p=128)),
                   writes=r_win_f, dma=True)
            P.emit('pool', lambda e: e.dma_start(out=wout_f, in_=wout_d[0:512, :].rearrange("(c p) f -> p c f", p=128)),
                   writes=r_wout_f, dma=True)
            for tb in range(4):
                uf, r_uf = ufT[tb % 2]
                for g in range(4):
                    bk = next_bank([0, 1, 2, 3])
                    for dc in range(8):
                        P.emit('pe', lambda e, bk=bk, dc=dc, g=g, tb=tb: e.matmul(
                            bank(bk), lhsT=win_f[:, dc, g * 128:(g + 1) * 128], rhs=hm[:, dc, tb * 512:(tb + 1) * 512],
                            start=(dc == 0), stop=(dc == 7)),
                            reads=r_win_f + [R_hm[dc][tb * 4 + t] for t in range(4)], writes=[R_bank[bk]])
                    evac(uf[:, g, :], bank(bk), [R_bank[bk]], r_uf)
                for tt in range(4):
                    T = tb * 4 + tt
                    pr = (4, 5) if T % 2 == 0 else (6, 7)
                    for g in range(4):
                        bk = pr[g // 2]
                        P.emit('pe', lambda e, bk=bk, g=g, tt=tt, uf=uf: e.matmul(
                            bank(bk)[:, (g % 2) * 256:(g % 2) * 256 + 256], lhsT=uf[:, g, tt * 128:(tt + 1) * 128],
                            rhs=ccsc[:], start=True, stop=True),
                            reads=r_uf + [R_ccsc], writes=[R_bank[bk]])
                    for hf in range(2):
                        evac(AB[:, T, 2 * hf:2 * hf + 2, :], bank(pr[hf]).rearrange("p (g f) -> p g f", g=2),
                             [R_bank[pr[hf]]], r_AB[T])
            step = 0
            for kh in range(2):
                for sc in range(16):
                    sbuf_, r_s = stream[step % 4]
                    step += 1
                    P.emit('sp', lambda e, sbuf_=sbuf_, sc=sc, kh=kh: e.dma_start(
                        out=sbuf_, in_=csmat_d[sc][:, :, kh * 1024:(kh + 1) * 1024]), writes=r_s, dma=True)
                    for g in range(4):
                        for m in range(2):
                            for j in range(2):
                                bk = g * 2 + j
                                P.emit('pe', lambda e, bk=bk, sc=sc, g=g, m=m, j=j, sbuf_=sbuf_: e.matmul(
                                    bank(bk), lhsT=AB[:, sc, g, m * 128:(m + 1) * 128],
                                    rhs=sbuf_[:, m, j * 512:(j + 1) * 512],
                                    start=(sc == 0 and m == 0), stop=(sc == 15 and m == 1)),
                                    reads=r_AB[sc] + r_s, writes=[R_bank[bk]])
                for g in range(4):
                    yt, r_yt = YT[g % 2]
                    for j in range(2):
                        evac(yt[:, j * 512:(j + 1) * 512], bank(g * 2 + j), [R_bank[g * 2 + j]], r_yt)
                    for j in range(2):
                        bk = g * 2 + j
                        P.emit('pe', lambda e, bk=bk, g=g, j=j, yt=yt: e.matmul(
                            bank(bk), lhsT=fnetw[:, g, :], rhs=yt[:, j * 512:(j + 1) * 512], start=True, stop=True),
                            reads=[R_fnetw] + r_yt, writes=[R_bank[bk]])
                        evac(foutT[:, g, j * 512:(j + 1) * 512], bank(bk), [R_bank[bk], R_fnetb], r_foutT,
                             bias=fnetb[:, g:g + 1])
                for t in range(8):
                    T = kh * 8 + t
                    for dh in range(2):
                        bk = next_bank([0, 1, 2, 3, 4, 5, 6, 7])
                        for g in range(4):
                            P.emit('pe', lambda e, bk=bk, g=g, t=t, dh=dh: e.matmul(
                                bank(bk), lhsT=foutT[:, g, t * 128:(t + 1) * 128],
                                rhs=wout_f[:, g, dh * 512:(dh + 1) * 512], start=(g == 0), stop=(g == 3)),
                                reads=r_foutT + r_wout_f, writes=[R_bank[bk]])
                        x_add(bk, T, dh)

        def mix_p3():
            P.emit('pool', lambda e: e.dma_start(out=win_q, in_=win_d[:, 512:1280].rearrange("(c p) f -> p c f", p=128)),
                   writes=r_win_q, dma=True)
            P.emit('sp', lambda e: e.dma_start(out=cos_t, in_=cos_d.rearrange("p (t f) -> p t f", t=NT)),
                   writes=r_cos, dma=True)
            P.emit('sp', lambda e: e.dma_start(out=sin_t, in_=sin_d.rearrange("p (t f) -> p t f", t=NT)),
                   writes=r_sin, dma=True)
            P.emit('pool', lambda e: e.memset(Vaug[:, :, :, 0:64], 1.0), writes=r_Vaug)
            P.emit('pool', lambda e: e.memset(Vaug[:, :, :, 128:192], 1.0), writes=r_Vaug)
            tile_banks = {}

            def stage_a_pe(T):
                bq, bkv = (0, 1) if T % 2 == 0 else (2, 3)
                tile_banks[T] = (bq, bkv)
                for dc in range(8):
                    P.emit('pe', lambda e, dc=dc, T=T, bq=bq: e.matmul(
                        bank(bq), lhsT=hm[:, dc, T * 128:(T + 1) * 128], rhs=win_q[:, dc, 0:512],
                        start=(dc == 0), stop=(dc == 7)), reads=[R_hm[dc][T]] + r_win_q, writes=[R_bank[bq]])
                for dc in range(8):
                    P.emit('pe', lambda e, dc=dc, T=T, bkv=bkv: e.matmul(
                        bank(bkv, 128, 256), lhsT=hm[:, dc, T * 128:(T + 1) * 128], rhs=win_q[:, dc, 512:768],
                        start=(dc == 0), stop=(dc == 7)), reads=[R_hm[dc][T]] + r_win_q, writes=[R_bank[bkv]])

            def stage_a(T):
                j = T % 2
                qk, r_qk = qkb[j]
                t1, r_t1 = t1b[j]
                bq, bkv = tile_banks[T]
                P.emit('act', lambda e, qk=qk, bq=bq: e.activation(out=qk[:, 0:512], in_=bank(bq), func=AF.Copy),
                       reads=[R_bank[bq]], writes=r_qk)
                P.emit('act', lambda e, qk=qk, bkv=bkv: e.activation(out=qk[:, 512:640], in_=bank(bkv, 128, 128), func=AF.Copy),
                       reads=[R_bank[bkv]], writes=r_qk)
                P.emit('act', lambda e, T=T, bkv=bkv: e.activation(
                    out=Vaug[:, T, :, 64:128], in_=bank(bkv, 128, 256)[:, 128:256].rearrange("p (k f) -> p k f", k=2),
                    func=AF.Copy), reads=[R_bank[bkv]], writes=r_Vaug)
                P.emit('dve', lambda e, qk=qk, t1=t1: e.tensor_tensor(out=t1[:], in0=qk[:], in1=qk[:], op=ALU.mult),
                       reads=r_qk, writes=r_t1)
                P.emit('dve', lambda e, t1=t1, j=j: e.tensor_reduce(
                    out=qss[j][:, 0:10], in_=t1[:].rearrange("p (h f) -> p h f", h=10), axis=AX.X, op=ALU.add),
                    reads=r_t1, writes=[R_qss[j]])
                P.emit('dve', lambda e, j=j: e.tensor_scalar(out=qv[j][:, 0:10], in0=qss[j][:, 0:10], scalar1=1.0 / 64,
                                                             scalar2=EPS, op0=ALU.mult, op1=ALU.add),
                       reads=[R_qss[j]], writes=[R_qv[j]])
                P.emit('act', lambda e, j=j: e.activation(out=ql[j][:, 0:10], in_=qv[j][:, 0:10], func=AF.Ln),
                       reads=[R_qv[j]], writes=[R_ql[j]])
                P.emit('act', lambda e, j=j: e.activation(out=qr[j][:, 0:10], in_=ql[j][:, 0:10], func=AF.Exp, scale=-0.5),
                       reads=[R_ql[j]], writes=[R_qr[j]])
                P.emit('pool', lambda e, qk=qk: e.tensor_tensor(out=qk[:], in0=qk[:], in1=gqk[:], op=ALU.mult),
                       reads=[R_gqk] + r_qk, writes=r_qk)

            def stage_b(T):
                j = T % 2
                qk, r_qk = qkb[j]
                t1, r_t1 = t1b[j]
                t2, r_t2 = t2b[j]
                cosb = cos_t[:, T, :].unsqueeze(1).broadcast_to([128, 10, 64])
                P.emit('dve', lambda e, qk=qk, t1=t1, cosb=cosb: e.tensor_tensor(
                    out=t1[:].rearrange("p (h f) -> p h f", h=10), in0=qk[:].rearrange("p (h f) -> p h f", h=10),
                    in1=cosb, op=ALU.mult), reads=r_qk + r_cos, writes=r_t1)
                for f in range(2):
                    sinb = sin_t[:, T, :].rearrange("p (a f d) -> p a f d", a=2, f=2)[:, :, f, :] \
                        .unsqueeze(1).broadcast_to([128, 10, 2, 16])
                    qsrc = qk[:].rearrange("p (h a f d) -> p h a f d", h=10, a=2, f=2)[:, :, :, 1 - f, :]
                    tdst = t2[:].rearrange("p (h a f d) -> p h a f d", h=10, a=2, f=2)[:, :, :, f, :]
                    P.emit('pool', lambda e, sinb=sinb, qsrc=qsrc, tdst=tdst: e.tensor_tensor(
                        out=tdst, in0=qsrc, in1=sinb, op=ALU.mult), reads=r_qk + r_sin, writes=r_t2)
                P.emit('dve', lambda e, t1=t1, t2=t2: e.tensor_tensor(out=t1[:], in0=t1[:], in1=t2[:], op=ALU.add),
                       reads=r_t2 + r_t1, writes=r_t1)
                rq_b = qr[j][:, 0:8].unsqueeze(2).broadcast_to([128, 8, 64])
                P.emit('dve', lambda e, t1=t1, j=j, rq_b=rq_b: e.tensor_tensor(
                    out=qrot[j][:].rearrange("p (h f) -> p h f", h=8),
                    in0=t1[:, 0:512].rearrange("p (h f) -> p h f", h=8), in1=rq_b, op=ALU.mult),
                    reads=r_t1 + [R_qr[j]], writes=[R_qrot[j]])
                rk_b = qr[j][:, 8:10].unsqueeze(2).broadcast_to([128, 2, 64])
                for dup in range(2):
                    P.emit('dve', lambda e, t1=t1, j=j, rk_b=rk_b, dup=dup: e.tensor_tensor(
                        out=krot[j][:, :, dup, :], in0=t1[:, 512:640].rearrange("p (h f) -> p h f", h=2),
                        in1=rk_b, op=ALU.mult), reads=r_t1 + [R_qr[j]], writes=[R_krot[j]])
                b = next_bank([4, 5, 6, 7])
                bv = bank(b).bitcast(BF16).rearrange("p (c t) -> p c t", c=8)
                for c in range(4):
                    P.emit('pe', lambda e, c=c, j=j, bv=bv: e.transpose(out=bv[:, c, :], in_=qrot[j][:, c * 128:(c + 1) * 128],
                                                                        identity=ident[:]),
                           reads=[R_qrot[j], R_ident], writes=[R_bank[b]])
                for kv in range(2):
                    P.emit('pe', lambda e, kv=kv, j=j, bv=bv: e.transpose(
                        out=bv[:, 4 + kv, :], in_=krot[j][:, kv, :, :].rearrange("p a f -> p (a f)"), identity=ident[:]),
                        reads=[R_krot[j], R_ident], writes=[R_bank[b]])
                evac(qT[:, :, T * 128:(T + 1) * 128], bv[:, 0:4, :], [R_bank[b]], r_qT)
                evac(kT2[:, :, T * 128:(T + 1) * 128], bv[:, 4:6, :], [R_bank[b]], r_kT2)

            stage_a_pe(0)
            stage_a_pe(1)
            stage_a(0)
            for T in range(NT):
                if T + 2 < NT:
                    stage_a_pe(T + 2)
                if T + 1 < NT:
                    stage_a(T + 1)
                stage_b(T)

        def mix_p4():
            P.emit('pool', lambda e: e.dma_start(out=wout_a, in_=wout_d[512:1024, :].rearrange("(c p) f -> p c f", p=128)),
                   writes=r_wout_a, dma=True)
            steps = []
            it = 0
            for hp in range(4):
                for qb in range(4):
                    acc = (4, 5) if it % 2 == 0 else (6, 7)
                    it += 1
                    for kc in range(16):
                        steps.append((hp, qb, kc, acc))

            def emit_st(i):
                hp, qb, kc, acc = steps[i]
                kv = hp // 2
                stp = (0, 1) if i % 2 == 0 else (2, 3)
                for par in range(2):
                    base = par * 64
                    P.emit('pe', lambda e, bk=stp[par], kv=kv, kc=kc, hp=hp, qb=qb, base=base: e.matmul(
                        bank(bk), lhsT=kT2[base:base + 64, kv, kc * 128:(kc + 1) * 128],
                        rhs=qT[base:base + 64, hp, qb * 512:(qb + 1) * 512],
                        start=True, stop=True), reads=r_kT2 + r_qT, writes=[R_bank[stp[par]]])

            def emit_rest(i):
                hp, qb, kc, acc = steps[i]
                kv = hp // 2
                stp = (0, 1) if i % 2 == 0 else (2, 3)
                pt, r_pt = PT[i % 3]
                P.emit('act', lambda e, pt=pt, sp_=stp[0] // 2: e.activation(
                    out=pt[:], in_=pp[sp_][:, :], func=AF.Exp, scale=0.125),
                    reads=[R_bank[stp[0]], R_bank[stp[1]]], writes=r_pt)
                for par in range(2):
                    va = Vaug[:, kc, kv, 64:192] if par == 0 else Vaug[:, kc, kv, 0:128]
                    P.emit('pe', lambda e, bk=acc[par], va=va, pt=pt, par=par, kc=kc: e.matmul(
                        bank(bk), lhsT=va, rhs=pt[:, par * 512:(par + 1) * 512],
                        start=(kc == 0), stop=(kc == 15)), reads=r_Vaug + r_pt, writes=[R_bank[acc[par]]])
                if kc == 15:
                    c0 = qb * 512
                    for par in range(2):
                        nb = par * 64
                        db = (1 - par) * 64
                        rc, r_rc = rec[par]
                        P.emit('dve', lambda e, rc=rc, bk=acc[par], nb=nb, db=db: e.reciprocal(
                            out=rc[nb:nb + 64, :], in_=bank(bk)[db:db + 64, :]),
                            reads=[R_bank[acc[par]]], writes=r_rc)
                        P.emit('dve', lambda e, rc=rc, bk=acc[par], nb=nb, hp=hp, c0=c0: e.tensor_tensor(
                            out=hm[nb:nb + 64, 4 + hp, c0:c0 + 512], in0=bank(bk)[nb:nb + 64, :],
                            in1=rc[nb:nb + 64, :], op=ALU.mult),
                            reads=[R_bank[acc[par]]] + r_rc, writes=[R_hm[4 + hp][c0 // 128 + t] for t in range(4)])

            emit_st(0)
            for i in range(len(steps)):
                if i + 1 < len(steps):
                    emit_st(i + 1)
                emit_rest(i)
            for T in range(NT):
                for dh in range(2):
                    bk = next_bank([0, 1, 2, 3])
                    for c in range(4):
                        P.emit('pe', lambda e, bk=bk, c=c, T=T, dh=dh: e.matmul(
                            bank(bk), lhsT=hm[:, 4 + c, T * 128:(T + 1) * 128], rhs=wout_a[:, c, dh * 512:(dh + 1) * 512],
                            start=(c == 0), stop=(c == 3)), reads=[R_hm[4 + c][T]] + r_wout_a, writes=[R_bank[bk]])
                    x_add(bk, T, dh)

        def dump_x(s):
            for T in range(NT):
                P.emit('sp', lambda e, T=T, s=s: e.dma_start(
                    out=y_d[s * S + T * 128:s * S + (T + 1) * 128, :], in_=x[:, T, :]), reads=R_x[T], dma=True, is_out=True)

        for s in range(NSEQ if stop is None else 1):
            g1 = load_gain("ffn1_norm")
            for T in range(NT):
                P.emit('sp', lambda e, T=T, s=s: e.dma_start(out=x[:, T, :], in_=x_d[s * S + T * 128:s * S + (T + 1) * 128, :]),
                       writes=R_x[T], dma=True)
            if s == 0:
                load_group(1, 0)
            if stop == 'w0':
                wg, r0, wu, r1, wd, r2 = wbuf_views(0)
                P.emit('pool', lambda e: e.dma_start(out=y_d[0:512, :].rearrange("(p r) f -> p (r f)", r=4),
                                                     in_=wg.rearrange("p c f -> p (c f)")), reads=r0, dma=True, is_out=True)
                P.emit('pool', lambda e: e.dma_start(out=y_d[512:1024, :].rearrange("(p r) f -> p (r f)", r=4),
                                                     in_=wd.rearrange("p c f -> p (c f)")), reads=r2, dma=True, is_out=True)
                break
            norm_to_hT(g1)
            ffn(1, pre_loaded=True)
            if stop == 'w4':
                wg, r0, wu, r1, wd, r2 = wbuf_views(0)
                P.emit('pool', lambda e: e.dma_start(out=y_d[0:512, :].rearrange("(p r) f -> p (r f)", r=4),
                                                     in_=wg.rearrange("p c f -> p (c f)")), reads=r0, dma=True, is_out=True)
                P.emit('pool', lambda e: e.dma_start(out=y_d[512:1024, :].rearrange("(p r) f -> p (r f)", r=4),
                                                     in_=wd.rearrange("p c f -> p (c f)")), reads=r2, dma=True, is_out=True)
                break
            if stop == 'ffn1':
                dump_x(s); break
            if 'skipmix' not in DBG:
                gm = load_gain("mix_norm")
                norm_to_hT(gm)
                mix_p1_p2()
                if stop == 'p2':
                    dump_x(s); break
                mix_p3()
                if 'nopre' not in DBG:
                    load_group(2, 0)
                mix_p4()
                if stop == 'mix':
                    dump_x(s); break
                if 'nopre' in DBG:
                    load_group(2, 0)
            else:
                load_group(2, 0)
            g2 = load_gain("ffn2_norm")
            norm_to_hT(g2)
            ffn(2, pre_loaded=True)
            if stop == 'ffn2':
                dump_x(s); break
            gf = load_gain("final_norm")
            if s + 1 < NSEQ:
                load_group(1, 0)
            rms_stats()
            for T in range(NT):
                ys, r_ys = ystage[T % 2]
                P.emit('dve', lambda e, T=T, ys=ys, gf=gf: e.scalar_tensor_tensor(
                    out=ys[:], in0=x[:, T, :], scalar=rstd[:, T:T + 1], in1=gbuf[gf][:], op0=ALU.mult, op1=ALU.mult),
                    reads=R_x[T] + [R_rstd, R_g[gf]], writes=r_ys)
                P.emit('sp', lambda e, T=T, s=s, ys=ys: e.dma_start(
                    out=y_d[s * S + T * 128:s * S + (T + 1) * 128, :], in_=ys[:]), reads=r_ys, dma=True, is_out=True)
        P.finish('sp')
        P.replay(st)
    return nc


def _constants():
    bf = ml_dtypes.bfloat16
    c = {}
    c["ident"] = np.eye(128, dtype=np.float32).astype(bf)
    k = np.arange(128)
    ang = 2.0 * np.pi * ((k[:, None] * k[None, :]) % 128) / 128.0
    c["ccsc"] = np.concatenate([np.cos(ang), np.sin(ang)], axis=1).astype(np.float32) / np.float32(math.sqrt(128.0))
    c["ccsc"] = c["ccsc"].astype(bf)
    n = np.arange(S)
    angs = 2.0 * np.pi * ((n[:, None] * n[None, :]) % S) / float(S)
    Cs = (np.cos(angs) / math.sqrt(S)).astype(np.float32)
    Ss = (-np.sin(angs) / math.sqrt(S)).astype(np.float32)
    cs = np.stack([Cs.reshape(16, 128, S), Ss.reshape(16, 128, S)], axis=2)
    c["csmat"] = np.ascontiguousarray(cs).astype(bf)
    t = np.arange(S)
    row = (t // 64).astype(np.float64)
    col = (t % 64).astype(np.float64)
    inv = 10000.0 ** (-np.arange(0, 32, 2, dtype=np.float64) / 32.0)
    ar = row[:, None] * inv[None, :]
    ac = col[:, None] * inv[None, :]
    cos64 = np.concatenate([np.cos(ar), np.cos(ar), np.cos(ac), np.cos(ac)], axis=1)
    sin64 = np.concatenate([-np.sin(ar), np.sin(ar), -np.sin(ac), np.sin(ac)], axis=1)
    c["cos_t"] = np.ascontiguousarray(cos64.reshape(NT, 128, 64).transpose(1, 0, 2).reshape(128, NT * 64)).astype(np.float32)
    c["sin_t"] = np.ascontiguousarray(sin64.reshape(NT, 128, 64).transpose(1, 0, 2).reshape(128, NT * 64)).astype(np.float32)
    return c


_CACHE = {}


def kernel(x, ffn1_norm, ffn1_w_gate, ffn1_w_up, ffn1_w_down, mix_norm, w_in, fnet_w, fnet_b, q_norm, k_norm,
           w_out, ffn2_norm, ffn2_w_gate, ffn2_w_up, ffn2_w_down, final_norm):
    f32 = lambda a: np.ascontiguousarray(np.asarray(a, dtype=np.float32))
    x = f32(x)
    if "nc" not in _CACHE:
        _CACHE["nc"] = build_program()
        _CACHE["const"] = _constants()
    nc = _CACHE["nc"]
    const = _CACHE["const"]
    shared = {
        "ffn1_norm": f32(ffn1_norm).reshape(1, D), "mix_norm": f32(mix_norm).reshape(1, D),
        "ffn2_norm": f32(ffn2_norm).reshape(1, D), "final_norm": f32(final_norm).reshape(1, D),
        "ffn1_w_gate": f32(ffn1_w_gate), "ffn1_w_up": f32(ffn1_w_up), "ffn1_w_down": f32(ffn1_w_down),
        "ffn2_w_gate": f32(ffn2_w_gate), "ffn2_w_up": f32(ffn2_w_up), "ffn2_w_down": f32(ffn2_w_down),
        "w_in": f32(w_in), "w_out": f32(w_out), "fnet_w": f32(fnet_w),
        "fnet_bT": np.ascontiguousarray(f32(fnet_b).T),
        "gqk": np.concatenate([np.tile(f32(q_norm), 8), np.tile(f32(k_norm), 2)]).reshape(1, 640),
    }
    shared.update(const)
    xs = x.reshape(N_CORES, NSEQ * S, D)
    in_maps = []
    for c in range(N_CORES):
        m = dict(shared)
        m["x"] = np.ascontiguousarray(xs[c])
        in_maps.append(m)
    res = run_bass_kernel_spmd(nc, in_maps, core_ids=list(range(N_CORES)))
    out = np.stack([np.asarray(r["y"], dtype=np.float32) for r in res.results], axis=0)
    return out.reshape(16, S, D)
```

```python
import math, os
DBG = os.environ.get('DBG', '')
from bisect import bisect_left
from contextlib import ExitStack

import numpy as np
import ml_dtypes
import concourse.bass as bass
import concourse.mybir as mybir
from concourse.bass_utils import run_bass_kernel_spmd

F32 = mybir.dt.float32
BF16 = mybir.dt.bfloat16
AF = mybir.ActivationFunctionType
ALU = mybir.AluOpType
AX = mybir.AxisListType

D = 1024
S = 2048
DFF = 2752
NSEQ = 2
NT = S // 128
EPS = 1e-6
N_CORES = 8

ENGS = ['pe', 'act', 'dve', 'pool', 'sp']
N_DMA_SEMS = 24


class Ticket:
    __slots__ = ('eng', 'idx', 'sem', 'val')

    def __init__(self, eng=None, idx=None, sem=None, val=None):
        self.eng = eng; self.idx = idx; self.sem = sem; self.val = val


class Res:
    __slots__ = ('w', 'r_eng', 'r_dma', 'excl')

    def __init__(self, excl=False):
        self.w = None
        self.r_eng = {}
        self.r_dma = []
        self.excl = excl


class Prog:
    def __init__(self, nc):
        self.nc = nc
        self.ops = {e: [] for e in ENGS}
        self.flag_idx = {e: [] for e in ENGS}
        self.flag_val = {e: [] for e in ENGS}
        self.seen = {e: {} for e in ENGS}
        self.dma_next = {'sp': 0, 'pool': 0}
        self.dma_val = {'sp': [0] * N_DMA_SEMS, 'pool': [0] * N_DMA_SEMS}
        self.out_tickets = []

    def resolve(self, t):
        if t.sem is not None:
            return (t.sem, t.val)
        e = t.eng
        fi = self.flag_idx[e]
        if fi and fi[-1] >= t.idx:
            k = bisect_left(fi, t.idx)
            return (('eng', e), self.flag_val[e][k])
        v = len(fi) + 1
        fi.append(t.idx)
        self.flag_val[e].append(v)
        self.ops[e][t.idx]['inc'] = (('eng', e), 1)
        return (('eng', e), v)

    def emit(self, eng, fn, reads=(), writes=(), dma=False, is_out=False):
        rd = []
        wr = list(writes)
        for r in reads:
            (wr if r.excl else rd).append(r)
        deps = []
        for r in rd:
            if r.w is not None:
                deps.append((r.w, True))
        for w in wr:
            if w.w is not None:
                deps.append((w.w, w.excl))
            for t in w.r_eng.values():
                deps.append((t, False))
            for t in w.r_dma:
                deps.append((t, False))
        waits = []
        seen = self.seen[eng]
        for t, raw in deps:
            if not dma and t.eng == eng and eng == 'pe':
                continue
            k, v = self.resolve(t)
            if seen.get(k, 0) >= v:
                continue
            seen[k] = v
            waits.append((k, v))
        idx = len(self.ops[eng])
        op = dict(fn=fn, waits=waits, inc=None)
        if dma:
            s = self.dma_next[eng]
            self.dma_next[eng] = (s + 1) % N_DMA_SEMS
            key = ('dma_' + eng, s)
            prev = self.dma_val[eng][s]
            if prev > 0 and seen.get(key, 0) < prev:
                seen[key] = prev
                waits.append((key, prev))
            self.dma_val[eng][s] = prev + 16
            op['inc'] = (key, 16)
            tk = Ticket(sem=key, val=prev + 16)
            if is_out:
                self.out_tickets.append(tk)
        else:
            tk = Ticket(eng=eng, idx=idx)
        self.ops[eng].append(op)
        for r in rd:
            if dma:
                r.r_dma.append(tk)
            else:
                r.r_eng[eng] = tk
        for w in wr:
            w.w = tk
            w.r_eng = {}
            w.r_dma = []
        return tk

    def finish(self, eng='sp'):
        waits = []
        seen = self.seen[eng]
        for t in self.out_tickets:
            k, v = self.resolve(t)
            if seen.get(k, 0) >= v:
                continue
            seen[k] = v
            waits.append((k, v))
        self.ops[eng].append(dict(fn=None, waits=waits, inc=None))

    def replay(self, st):
        nc = self.nc
        sems = {}
        for e in ENGS:
            sems[('eng', e)] = st.enter_context(nc.semaphore('p_' + e))
        for q in ('sp', 'pool'):
            for s in range(N_DMA_SEMS):
                sems[('dma_' + q, s)] = st.enter_context(nc.semaphore('d%s%d' % (q, s)))
        block = st.enter_context(nc.Block())

        def mk(name):
            ops = self.ops[name]

            def body(engine):
                for op in ops:
                    for (k, v) in op['waits']:
                        engine.wait_ge(sems[k], v)
                    if op['fn'] is None:
                        continue
                    ins = op['fn'](engine)
                    if op['inc'] is not None:
                        ins.then_inc(sems[op['inc'][0]], op['inc'][1])
            return body

        block.tensor(mk('pe'))
        block.scalar(mk('act'))
        block.vector(mk('dve'))
        block.gpsimd(mk('pool'))
        block.sync(mk('sp'))


KB = 1024
ARENA_KB = 86


def build_program(stop=None):
    nc = bass.Bass("TRN2", target_bir_lowering=False)
    dt_in = lambda name, shape, dt=F32: nc.dram_tensor(name, shape, dt, kind="ExternalInput").ap()
    x_d = dt_in("x", [NSEQ * S, D])
    y_d = nc.dram_tensor("y", [NSEQ * S, D], F32, kind="ExternalOutput").ap()
    g_d = {n: dt_in(n, [1, D]) for n in ("ffn1_norm", "mix_norm", "ffn2_norm", "final_norm")}
    wgate_d = {1: dt_in("ffn1_w_gate", [D, DFF]), 2: dt_in("ffn2_w_gate", [D, DFF])}
    wup_d = {1: dt_in("ffn1_w_up", [D, DFF]), 2: dt_in("ffn2_w_up", [D, DFF])}
    wdown_d = {1: dt_in("ffn1_w_down", [DFF, D]), 2: dt_in("ffn2_w_down", [DFF, D])}
    win_d = dt_in("w_in", [D, 1280])
    wout_d = dt_in("w_out", [D, D])
    fnetw_d = dt_in("fnet_w", [4, 128, 128])
    fnetb_d = dt_in("fnet_bT", [128, 4])
    gqk_d = dt_in("gqk", [1, 640])
    ident_d = dt_in("ident", [128, 128], BF16)
    ccsc_d = dt_in("ccsc", [128, 256], BF16)
    csmat_d = dt_in("csmat", [16, 128, 2, S], BF16)
    cos_d = dt_in("cos_t", [128, NT * 64])
    sin_d = dt_in("sin_t", [128, NT * 64])

    P = Prog(nc)
    with ExitStack() as st:
        sb = lambda name, shape, dt: st.enter_context(nc.sbuf_tensor(name, shape, dt))
        x = sb("x_sb", [128, NT, D], F32)
        R_x = [[Res(), Res()] for _ in range(NT)]
        hm = sb("hm", [128, 8, S], BF16)
        R_hm = [[Res() for _ in range(NT)] for _ in range(8)]
        arena = sb("arena", [128, ARENA_KB * KB // 2], BF16)
        R_pg = [Res() for _ in range(ARENA_KB)]

        def aview(off_b, size_b, dt=BF16):
            ap = arena[:, off_b // 2:(off_b + size_b) // 2]
            if dt == F32:
                ap = ap.bitcast(F32)
            p0 = off_b // KB
            p1 = (off_b + size_b + KB - 1) // KB
            return ap, R_pg[p0:p1]

        gbuf = [sb("gbuf%d" % i, [128, D], F32) for i in range(2)]
        R_g = [Res() for _ in range(2)]
        gqk = sb("gqk_sb", [128, 640], F32); R_gqk = Res()
        ident = sb("ident_sb", [128, 128], BF16); R_ident = Res()
        ccsc = sb("ccsc_sb", [128, 256], BF16); R_ccsc = Res()
        fnetw = sb("fnetw_sb", [128, 4, 128], BF16); R_fnetw = Res()
        fnetb = sb("fnetb_sb", [128, 4], F32); R_fnetb = Res()
        hbuf = [sb("hbuf%d" % i, [128, D], BF16) for i in range(2)]
        R_h = [Res() for _ in range(2)]
        junk = sb("junk", [128, D], BF16); R_junk = Res()
        ssq = sb("ssq", [128, NT], F32); R_ssq = [Res() for _ in range(NT)]
        sv = sb("sv", [128, NT], F32); R_sv = Res()
        slog = sb("slog", [128, NT], F32); R_slog = Res()
        rstd = sb("rstd", [128, NT], F32); R_rstd = Res()
        qss = [sb("qss%d" % i, [128, 16], F32) for i in range(2)]; R_qss = [Res() for _ in range(2)]
        qv = [sb("qv%d" % i, [128, 16], F32) for i in range(2)]; R_qv = [Res() for _ in range(2)]
        ql = [sb("ql%d" % i, [128, 16], F32) for i in range(2)]; R_ql = [Res() for _ in range(2)]
        qr = [sb("qr%d" % i, [128, 16], F32) for i in range(2)]; R_qr = [Res() for _ in range(2)]
        qrot = [sb("qrot%d" % i, [128, 512], BF16) for i in range(2)]; R_qrot = [Res() for _ in range(2)]
        krot = [sb("krot%d" % i, [128, 2, 2, 64], BF16) for i in range(2)]; R_krot = [Res() for _ in range(2)]

        pp = [st.enter_context(nc.psum_tensor("pp%d" % i, [128, 1024], F32)) for i in range(4)]
        R_bank = [Res(excl=True) for _ in range(8)]

        def bank(k, rows=128, cols=512):
            return pp[k // 2][0:rows, (k % 2) * 512:(k % 2) * 512 + cols]

        rot = [0]

        def next_bank(cands):
            b = cands[rot[0] % len(cands)]
            rot[0] += 1
            return b

        evac_rot = [0]

        def evac(out_ap, in_ap, reads, writes, bias=None, force_act=False):
            if not force_act:
                evac_rot[0] += 1
            if bias is not None or force_act or evac_rot[0] % 2 == 0:
                if bias is not None:
                    P.emit('act', lambda e: e.activation(out=out_ap, in_=in_ap, func=AF.Identity, bias=bias),
                           reads=reads, writes=writes)
                else:
                    P.emit('act', lambda e: e.activation(out=out_ap, in_=in_ap, func=AF.Copy),
                           reads=reads, writes=writes)
            else:
                P.emit('dve', lambda e: e.tensor_copy(out=out_ap, in_=in_ap), reads=reads, writes=writes)

        P.emit('sp', lambda e: e.dma_start(out=ident[:], in_=ident_d), writes=[R_ident], dma=True)
        P.emit('sp', lambda e: e.dma_start(out=ccsc[:], in_=ccsc_d), writes=[R_ccsc], dma=True)
        P.emit('sp', lambda e: e.dma_start(out=fnetb[:], in_=fnetb_d), writes=[R_fnetb], dma=True)
        P.emit('sp', lambda e: e.dma_start(out=gqk[:], in_=gqk_d.partition_broadcast(128)), writes=[R_gqk], dma=True)
        P.emit('pool', lambda e: e.dma_start(out=fnetw[:], in_=fnetw_d.rearrange("g c d -> c g d")),
               writes=[R_fnetw], dma=True)

        gsel = [0]

        def load_gain(name):
            i = gsel[0] % 2
            gsel[0] += 1
            P.emit('sp', lambda e: e.dma_start(out=gbuf[i][:], in_=g_d[name].partition_broadcast(128)),
                   writes=[R_g[i]], dma=True)
            return i

        def rms_stats():
            for T in range(NT):
                P.emit('act', lambda e, T=T: e.activation(out=junk[:], in_=x[:, T, :], func=AF.Square,
                                                          accum_out=ssq[:, T:T + 1]),
                       reads=R_x[T], writes=[R_junk, R_ssq[T]])
            P.emit('dve', lambda e: e.tensor_scalar(out=sv[:], in0=ssq[:], scalar1=1.0 / D, scalar2=EPS,
                                                    op0=ALU.mult, op1=ALU.add), reads=R_ssq, writes=[R_sv])
            P.emit('act', lambda e: e.activation(out=slog[:], in_=sv[:], func=AF.Ln), reads=[R_sv], writes=[R_slog])
            P.emit('act', lambda e: e.activation(out=rstd[:], in_=slog[:], func=AF.Exp, scale=-0.5),
                   reads=[R_slog], writes=[R_rstd])

        def norm_to_hT(gi):
            rms_stats()
            for T in range(NT):
                j = T % 2
                P.emit('dve', lambda e, T=T, j=j: e.scalar_tensor_tensor(
                    out=hbuf[j][:], in0=x[:, T, :], scalar=rstd[:, T:T + 1], in1=gbuf[gi][:],
                    op0=ALU.mult, op1=ALU.mult), reads=R_x[T] + [R_rstd, R_g[gi]], writes=[R_h[j]])
                b = next_bank([4, 5, 6, 7])
                bv = bank(b).bitcast(BF16).rearrange("p (c t) -> p c t", c=8)
                for c in range(8):
                    P.emit('pe', lambda e, c=c, j=j, bv=bv: e.transpose(out=bv[:, c, :], in_=hbuf[j][:, c * 128:(c + 1) * 128],
                                                                        identity=ident[:]),
                           reads=[R_h[j], R_ident], writes=[R_bank[b]])
                evac(hm[:, :, T * 128:(T + 1) * 128], bv, [R_bank[b]], [R_hm[c][T] for c in range(8)], force_act=True)

        chunks = [(i * 128, 128) for i in range(21)] + [(21 * 128, 64)]
        groups = [chunks[i:i + 4] for i in range(0, 22, 4)]

        def wbuf_views(b):
            base = b * 24 * KB
            wg, r0 = aview(base, 8 * KB)
            wu, r1 = aview(base + 8 * KB, 8 * KB)
            wd, r2 = aview(base + 16 * KB, 8 * KB)
            return (wg.rearrange("p (c f) -> p c f", c=8), r0, wu.rearrange("p (c f) -> p c f", c=8), r1,
                    wd.rearrange("p (c f) -> p c f", c=4), r2)

        def load_group(which, gi):
            grp = groups[gi]
            b = gi % 2
            wg, r0, wu, r1, wd, r2 = wbuf_views(b)
            f0 = grp[0][0]
            width = sum(c[1] for c in grp)
            P.emit('pool', lambda e: e.dma_start(out=wg[:, :, 0:width],
                                                 in_=wgate_d[which][:, f0:f0 + width].rearrange("(c p) f -> p c f", p=128)),
                   writes=r0, dma=True)
            P.emit('pool', lambda e: e.dma_start(out=wu[:, :, 0:width],
                                                 in_=wup_d[which][:, f0:f0 + width].rearrange("(c p) f -> p c f", p=128)),
                   writes=r1, dma=True)
            nfull = sum(1 for c in grp if c[1] == 128)
            if nfull:
                P.emit('pool', lambda e: e.dma_start(out=wd[:, 0:nfull, :],
                                                     in_=wdown_d[which][f0:f0 + nfull * 128, :].rearrange("(c p) d -> p c d", p=128)),
                       writes=r2, dma=True)
            if nfull < len(grp):
                fo = f0 + nfull * 128
                P.emit('pool', lambda e: e.dma_start(out=wd[0:64, nfull, :], in_=wdown_d[which][fo:fo + 64, :]),
                       writes=r2, dma=True)

        actbuf = []
        for j in range(2):
            a, r = aview(48 * KB + j * 4 * KB, 4 * KB)
            actbuf.append((a.rearrange("p (c t) -> p c t", c=4), r))
        silubuf = [aview(56 * KB + j * 2 * KB, 2 * KB, F32) for j in range(2)]

        def ffn(which, pre_loaded):
            cnt = 0
            for gi, grp in enumerate(groups):
                if gi == 0 and not pre_loaded:
                    load_group(which, 0)
                if gi + 1 < len(groups):
                    load_group(which, gi + 1)
                b = gi % 2
                wg, r0, wu, r1, wd, r2 = wbuf_views(b)
                for tb in range(4):
                    ab, r_ab = actbuf[tb % 2]
                    hT_res = lambda dc: [R_hm[dc][tb * 4 + t] for t in range(4)]
                    for ci, (fo, fsz) in enumerate(grp):
                        pair = (0, 1) if cnt % 2 == 0 else (2, 3)
                        cnt += 1
                        for (bk, wv, rw) in ((pair[0], wg, r0), (pair[1], wu, r1)):
                            for dc in range(8):
                                P.emit('pe', lambda e, bk=bk, wv=wv, dc=dc, ci=ci, fsz=fsz, tb=tb: e.matmul(
                                    bank(bk, fsz), lhsT=wv[:, dc, ci * 128:ci * 128 + fsz],
                                    rhs=hm[:, dc, tb * 512:(tb + 1) * 512], start=(dc == 0), stop=(dc == 7)),
                                    reads=rw + hT_res(dc), writes=[R_bank[bk]])
                        sl, r_sl = silubuf[cnt % 2]
                        P.emit('act', lambda e, sl=sl, fsz=fsz, bk=pair[0]: e.activation(
                            out=sl[0:fsz, :], in_=bank(bk, fsz), func=AF.Silu),
                            reads=[R_bank[pair[0]]], writes=r_sl)
                        P.emit('dve', lambda e, sl=sl, fsz=fsz, bk=pair[1], ab=ab, ci=ci: e.tensor_tensor(
                            out=ab[0:fsz, ci, :], in0=bank(bk, fsz), in1=sl[0:fsz, :], op=ALU.mult),
                            reads=[R_bank[pair[1]]] + r_sl, writes=r_ab)
                    for tt in range(4):
                        T = tb * 4 + tt
                        for dh in range(2):
                            bk = next_bank([4, 5, 6, 7])
                            for ci, (fo, fsz) in enumerate(grp):
                                P.emit('pe', lambda e, bk=bk, ab=ab, ci=ci, fsz=fsz, tt=tt, dh=dh, wd=wd, last=(ci == len(grp) - 1): e.matmul(
                                    bank(bk), lhsT=ab[0:fsz, ci, tt * 128:(tt + 1) * 128],
                                    rhs=wd[0:fsz, ci, dh * 512:(dh + 1) * 512],
                                    start=(ci == 0), stop=last),
                                    reads=r_ab + r2, writes=[R_bank[bk]])
                            P.emit('dve', lambda e, bk=bk, T=T, dh=dh: e.scalar_tensor_tensor(
                                out=x[:, T, dh * 512:(dh + 1) * 512], in0=bank(bk), scalar=0.5,
                                in1=x[:, T, dh * 512:(dh + 1) * 512], op0=ALU.mult, op1=ALU.add),
                                reads=[R_bank[bk], R_x[T][dh]], writes=[R_x[T][dh]])

        win_f, r_win_f = aview(0, 8 * KB); win_f = win_f.rearrange("p (c f) -> p c f", c=8)
        wout_f, r_wout_f = aview(8 * KB, 8 * KB); wout_f = wout_f.rearrange("p (c f) -> p c f", c=4)
        AB, r_AB_all = aview(16 * KB, 32 * KB); AB = AB.rearrange("p (t g f) -> p t g f", t=NT, g=4)
        r_AB = [R_pg[16 + 2 * T:18 + 2 * T] for T in range(NT)]
        stream = []
        for k in range(4):
            a, r = aview(48 * KB + k * 4 * KB, 4 * KB)
            stream.append((a.rearrange("p (m k) -> p m k", m=2), r))
        ufT = []
        for j in range(2):
            a, r = aview(64 * KB + j * 4 * KB, 4 * KB)
            ufT.append((a.rearrange("p (g t) -> p g t", g=4), r))
        YT = [aview(64 * KB + j * 2 * KB, 2 * KB) for j in range(2)]
        foutT, r_foutT = aview(68 * KB, 8 * KB); foutT = foutT.rearrange("p (g t) -> p g t", g=4)
        win_q, r_win_q = aview(0, 12 * KB); win_q = win_q.rearrange("p (c f) -> p c f", c=8)
        cos_t, r_cos = aview(12 * KB, 4 * KB, F32); cos_t = cos_t.rearrange("p (t f) -> p t f", t=NT)
        sin_t, r_sin = aview(16 * KB, 4 * KB, F32); sin_t = sin_t.rearrange("p (t f) -> p t f", t=NT)
        qT, r_qT = aview(24 * KB, 16 * KB); qT = qT.rearrange("p (c t) -> p c t", c=4)
        kT2, r_kT2 = aview(40 * KB, 8 * KB); kT2 = kT2.rearrange("p (c t) -> p c t", c=2)
        Vaug, r_Vaug = aview(48 * KB, 12 * KB); Vaug = Vaug.rearrange("p (t k f) -> p t k f", t=NT, k=2)
        qkb = [aview(60 * KB + j * 3 * KB, 2560, F32) for j in range(2)]
        t1b = [aview(66 * KB + j * 3 * KB, 2560, F32) for j in range(2)]
        t2b = [aview(72 * KB + j * 3 * KB, 2560, F32) for j in range(2)]
        wout_a, r_wout_a = aview(78 * KB, 8 * KB); wout_a = wout_a.rearrange("p (c f) -> p c f", c=4)
        PT = [aview(60 * KB + j * 2 * KB, 2 * KB) for j in range(3)]
        rec = [aview(66 * KB + j * 2 * KB, 2 * KB, F32) for j in range(2)]
        ystage = [aview(48 * KB + j * 4 * KB, 4 * KB, F32) for j in range(2)]

        def x_add(bk, T, dh):
            P.emit('dve', lambda e: e.tensor_tensor(out=x[:, T, dh * 512:(dh + 1) * 512], in0=bank(bk),
                                                    in1=x[:, T, dh * 512:(dh + 1) * 512], op=ALU.add),
                   reads=[R_bank[bk], R_x[T][dh]], writes=[R_x[T][dh]])

        def mix_p1_p2():
            P.emit('pool', lambda e: e.dma_start(out=win_f, in_=win_d[:, 0:512].rearrange("(c p) f -> p c f", p=128)),
                   writes=r_win_f, dma=True)
            P.emit('pool', lambda e: e.dma_start(out=wout_f, in_=wout_d[0:512, :].rearrange("(c p) f -> p c f", p=128)),
                   writes=r_wout_f, dma=True)
            for tb in range(4):
                uf, r_uf = ufT[tb % 2]
                for g in range(4):
                    bk = next_bank([0, 1, 2, 3])
                    for dc in range(8):
                        P.emit('pe', lambda e, bk=bk, dc=dc, g=g, tb=tb: e.matmul(
                            bank(bk), lhsT=win_f[:, dc, g * 128:(g + 1) * 128], rhs=hm[:, dc, tb * 512:(tb + 1) * 512],
                            start=(dc == 0), stop=(dc == 7)),
                            reads=r_win_f + [R_hm[dc][tb * 4 + t] for t in range(4)], writes=[R_bank[bk]])
                    evac(uf[:, g, :], bank(bk), [R_bank[bk]], r_uf)
                for tt in range(4):
                    T = tb * 4 + tt
                    pr = (4, 5) if T % 2 == 0 else (6, 7)
                    for g in range(4):
                        bk = pr[g // 2]
                        P.emit('pe', lambda e, bk=bk, g=g, tt=tt, uf=uf: e.matmul(
                            bank(bk)[:, (g % 2) * 256:(g % 2) * 256 + 256], lhsT=uf[:, g, tt * 128:(tt + 1) * 128],
                            rhs=ccsc[:], start=True, stop=True),
                            reads=r_uf + [R_ccsc], writes=[R_bank[bk]])
                    for hf in range(2):
                        evac(AB[:, T, 2 * hf:2 * hf + 2, :], bank(pr[hf]).rearrange("p (g f) -> p g f", g=2),
                             [R_bank[pr[hf]]], r_AB[T])
            step = 0
            for kh in range(2):
                for sc in range(16):
                    sbuf_, r_s = stream[step % 4]
                    step += 1
                    P.emit('sp', lambda e, sbuf_=sbuf_, sc=sc, kh=kh: e.dma_start(
                        out=sbuf_, in_=csmat_d[sc][:, :, kh * 1024:(kh + 1) * 1024]), writes=r_s, dma=True)
                    for g in range(4):
                        for m in range(2):
                            for j in range(2):
                                bk = g * 2 + j
                                P.emit('pe', lambda e, bk=bk, sc=sc, g=g, m=m, j=j, sbuf_=sbuf_: e.matmul(
                                    bank(bk), lhsT=AB[:, sc, g, m * 128:(m + 1) * 128],
                                    rhs=sbuf_[:, m, j * 512:(j + 1) * 512],
                                    start=(sc == 0 and m == 0), stop=(sc == 15 and m == 1)),
                                    reads=r_AB[sc] + r_s, writes=[R_bank[bk]])
                for g in range(4):
                    yt, r_yt = YT[g % 2]
                    for j in range(2):
                        evac(yt[:, j * 512:(j + 1) * 512], bank(g * 2 + j), [R_bank[g * 2 + j]], r_yt)
                    for j in range(2):
                        bk = g * 2 + j
                        P.emit('pe', lambda e, bk=bk, g=g, j=j, yt=yt: e.matmul(
                            bank(bk), lhsT=fnetw[:, g, :], rhs=yt[:, j * 512:(j + 1) * 512], start=True, stop=True),
                            reads=[R_fnetw] + r_yt, writes=[R_bank[bk]])
                        evac(foutT[:, g, j * 512:(j + 1) * 512], bank(bk), [R_bank[bk], R_fnetb], r_foutT,
                             bias=fnetb[:, g:g + 1])
                for t in range(8):
                    T = kh * 8 + t
                    for dh in range(2):
                        bk = next_bank([0, 1, 2, 3, 4, 5, 6, 7])
                        for g in range(4):
                            P.emit('pe', lambda e, bk=bk, g=g, t=t, dh=dh: e.matmul(
                                bank(bk), lhsT=foutT[:, g, t * 128:(t + 1) * 128],
                                rhs=wout_f[:, g, dh * 512:(dh + 1) * 512], start=(g == 0), stop=(g == 3)),
                                reads=r_foutT + r_wout_f, writes=[R_bank[bk]])
                        x_add(bk, T, dh)

        def mix_p3():
            P.emit('pool', lambda e: e.dma_start(out=win_q, in_=win_d[:, 512:1280].rearrange("(c p) f -> p c f", p=128)),
                   writes=r_win_q, dma=True)
            P.emit('sp', lambda e: e.dma_start(out=cos_t, in_=cos_d.rearrange("p (t f) -> p t f", t=NT)),
                   writes=r_cos, dma=True)
            P.emit('sp', lambda e: e.dma_start(out=sin_t, in_=sin_d.rearrange("p (t f) -> p t f", t=NT)),
                   writes=r_sin, dma=True)
            P.emit('pool', lambda e: e.memset(Vaug[:, :, :, 0:64], 1.0), writes=r_Vaug)
            P.emit('pool', lambda e: e.memset(Vaug[:, :, :, 128:192], 1.0), writes=r_Vaug)
            tile_banks = {}

            def stage_a_pe(T):
                bq, bkv = (0, 1) if T % 2 == 0 else (2, 3)
                tile_banks[T] = (bq, bkv)
                for dc in range(8):
                    P.emit('pe', lambda e, dc=dc, T=T, bq=bq: e.matmul(
                        bank(bq), lhsT=hm[:, dc, T * 128:(T + 1) * 128], rhs=win_q[:, dc, 0:512],
                        start=(dc == 0), stop=(dc == 7)), reads=[R_hm[dc][T]] + r_win_q, writes=[R_bank[bq]])
                for dc in range(8):
                    P.emit('pe', lambda e, dc=dc, T=T, bkv=bkv: e.matmul(
                        bank(bkv, 128, 256), lhsT=hm[:, dc, T * 128:(T + 1) * 128], rhs=win_q[:, dc, 512:768],
                        start=(dc == 0), stop=(dc == 7)), reads=[R_hm[dc][T]] + r_win_q, writes=[R_bank[bkv]])

            def stage_a(T):
                j = T % 2
                qk, r_qk = qkb[j]
                t1, r_t1 = t1b[j]
                bq, bkv = tile_banks[T]
                P.emit('act', lambda e, qk=qk, bq=bq: e.activation(out=qk[:, 0:512], in_=bank(bq), func=AF.Copy),
                       reads=[R_bank[bq]], writes=r_qk)
                P.emit('act', lambda e, qk=qk, bkv=bkv: e.activation(out=qk[:, 512:640], in_=bank(bkv, 128, 128), func=AF.Copy),
                       reads=[R_bank[bkv]], writes=r_qk)
                P.emit('act', lambda e, T=T, bkv=bkv: e.activation(
                    out=Vaug[:, T, :, 64:128], in_=bank(bkv, 128, 256)[:, 128:256].rearrange("p (k f) -> p k f", k=2),
                    func=AF.Copy), reads=[R_bank[bkv]], writes=r_Vaug)
                P.emit('dve', lambda e, qk=qk, t1=t1: e.tensor_tensor(out=t1[:], in0=qk[:], in1=qk[:], op=ALU.mult),
                       reads=r_qk, writes=r_t1)
                P.emit('dve', lambda e, t1=t1, j=j: e.tensor_reduce(
                    out=qss[j][:, 0:10], in_=t1[:].rearrange("p (h f) -> p h f", h=10), axis=AX.X, op=ALU.add),
                    reads=r_t1, writes=[R_qss[j]])
                P.emit('dve', lambda e, j=j: e.tensor_scalar(out=qv[j][:, 0:10], in0=qss[j][:, 0:10], scalar1=1.0 / 64,
                                                             scalar2=EPS, op0=ALU.mult, op1=ALU.add),
                       reads=[R_qss[j]], writes=[R_qv[j]])
                P.emit('act', lambda e, j=j: e.activation(out=ql[j][:, 0:10], in_=qv[j][:, 0:10], func=AF.Ln),
                       reads=[R_qv[j]], writes=[R_ql[j]])
                P.emit('act', lambda e, j=j: e.activation(out=qr[j][:, 0:10], in_=ql[j][:, 0:10], func=AF.Exp, scale=-0.5),
                       reads=[R_ql[j]], writes=[R_qr[j]])
                P.emit('pool', lambda e, qk=qk: e.tensor_tensor(out=qk[:], in0=qk[:], in1=gqk[:], op=ALU.mult),
                       reads=[R_gqk] + r_qk, writes=r_qk)

            def stage_b(T):
                j = T % 2
                qk, r_qk = qkb[j]
                t1, r_t1 = t1b[j]
                t2, r_t2 = t2b[j]
                cosb = cos_t[:, T, :].unsqueeze(1).broadcast_to([128, 10, 64])
                P.emit('dve', lambda e, qk=qk, t1=t1, cosb=cosb: e.tensor_tensor(
                    out=t1[:].rearrange("p (h f) -> p h f", h=10), in0=qk[:].rearrange("p (h f) -> p h f", h=10),
                    in1=cosb, op=ALU.mult), reads=r_qk + r_cos, writes=r_t1)
                for f in range(2):
                    sinb = sin_t[:, T, :].rearrange("p (a f d) -> p a f d", a=2, f=2)[:, :, f, :] \
                        .unsqueeze(1).broadcast_to([128, 10, 2, 16])
                    qsrc = qk[:].rearrange("p (h a f d) -> p h a f d", h=10, a=2, f=2)[:, :, :, 1 - f, :]
                    tdst = t2[:].rearrange("p (h a f d) -> p h a f d", h=10, a=2, f=2)[:, :, :, f, :]
                    P.emit('pool', lambda e, sinb=sinb, qsrc=qsrc, tdst=tdst: e.tensor_tensor(
                        out=tdst, in0=qsrc, in1=sinb, op=ALU.mult), reads=r_qk + r_sin, writes=r_t2)
                P.emit('dve', lambda e, t1=t1, t2=t2: e.tensor_tensor(out=t1[:], in0=t1[:], in1=t2[:], op=ALU.add),
                       reads=r_t2 + r_t1, writes=r_t1)
                rq_b = qr[j][:, 0:8].unsqueeze(2).broadcast_to([128, 8, 64])
                P.emit('dve', lambda e, t1=t1, j=j, rq_b=rq_b: e.tensor_tensor(
                    out=qrot[j][:].rearrange("p (h f) -> p h f", h=8),
                    in0=t1[:, 0:512].rearrange("p (h f) -> p h f", h=8), in1=rq_b, op=ALU.mult),
                    reads=r_t1 + [R_qr[j]], writes=[R_qrot[j]])
                rk_b = qr[j][:, 8:10].unsqueeze(2).broadcast_to([128, 2, 64])
                for dup in range(2):
                    P.emit('dve', lambda e, t1=t1, j=j, rk_b=rk_b, dup=dup: e.tensor_tensor(
                        out=krot[j][:, :, dup, :], in0=t1[:, 512:640].rearrange("p (h f) -> p h f", h=2),
                        in1=rk_b, op=ALU.mult), reads=r_t1 + [R_qr[j]], writes=[R_krot[j]])
                b = next_bank([4, 5, 6, 7])
                bv = bank(b).bitcast(BF16).rearrange("p (c t) -> p c t", c=8)
                for c in range(4):
                    P.emit('pe', lambda e, c=c, j=j, bv=bv: e.transpose(out=bv[:, c, :], in_=qrot[j][:, c * 128:(c + 1) * 128],
                                                                        identity=ident[:]),
                           reads=[R_qrot[j], R_ident], writes=[R_bank[b]])
                for kv in range(2):
                    P.emit('pe', lambda e, kv=kv, j=j, bv=bv: e.transpose(
                        out=bv[:, 4 + kv, :], in_=krot[j][:, kv, :, :].rearrange("p a f -> p (a f)"), identity=ident[:]),
                        reads=[R_krot[j], R_ident], writes=[R_bank[b]])
                evac(qT[:, :, T * 128:(T + 1) * 128], bv[:, 0:4, :], [R_bank[b]], r_qT)
                evac(kT2[:, :, T * 128:(T + 1) * 128], bv[:, 4:6, :], [R_bank[b]], r_kT2)

            stage_a_pe(0)
            stage_a_pe(1)
            stage_a(0)
            for T in range(NT):
                if T + 2 < NT:
                    stage_a_pe(T + 2)
                if T + 1 < NT:
                    stage_a(T + 1)
                stage_b(T)

        def mix_p4():
            P.emit('pool', lambda e: e.dma_start(out=wout_a, in_=wout_d[512:1024, :].rearrange("(c p) f -> p c f", p=128)),
                   writes=r_wout_a, dma=True)
            steps = []
            it = 0
            for hp in range(4):
                for qb in range(4):
                    acc = (4, 5) if it % 2 == 0 else (6, 7)
                    it += 1
                    for kc in range(16):
                        steps.append((hp, qb, kc, acc))

            def emit_st(i):
                hp, qb, kc, acc = steps[i]
                kv = hp // 2
                stp = (0, 1) if i % 2 == 0 else (2, 3)
                for par in range(2):
                    base = par * 64
                    P.emit('pe', lambda e, bk=stp[par], kv=kv, kc=kc, hp=hp, qb=qb, base=base: e.matmul(
                        bank(bk), lhsT=kT2[base:base + 64, kv, kc * 128:(kc + 1) * 128],
                        rhs=qT[base:base + 64, hp, qb * 512:(qb + 1) * 512],
                        start=True, stop=True), reads=r_kT2 + r_qT, writes=[R_bank[stp[par]]])

            def emit_rest(i):
                hp, qb, kc, acc = steps[i]
                kv = hp // 2
                stp = (0, 1) if i % 2 == 0 else (2, 3)
                pt, r_pt = PT[i % 3]
                P.emit('act', lambda e, pt=pt, sp_=stp[0] // 2: e.activation(
                    out=pt[:], in_=pp[sp_][:, :], func=AF.Exp, scale=0.125),
                    reads=[R_bank[stp[0]], R_bank[stp[1]]], writes=r_pt)
                for par in range(2):
                    va = Vaug[:, kc, kv, 64:192] if par == 0 else Vaug[:, kc, kv, 0:128]
                    P.emit('pe', lambda e, bk=acc[par], va=va, pt=pt, par=par, kc=kc: e.matmul(
                        bank(bk), lhsT=va, rhs=pt[:, par * 512:(par + 1) * 512],
                        start=(kc == 0), stop=(kc == 15)), reads=r_Vaug + r_pt, writes=[R_bank[acc[par]]])
                if kc == 15:
                    c0 = qb * 512
                    for par in range(2):
                        nb = par * 64
                        db = (1 - par) * 64
                        rc, r_rc = rec[par]
                        P.emit('dve', lambda e, rc=rc, bk=acc[par], nb=nb, db=db: e.reciprocal(
                            out=rc[nb:nb + 64, :], in_=bank(bk)[db:db + 64, :]),
                            reads=[R_bank[acc[par]]], writes=r_rc)
                        P.emit('dve', lambda e, rc=rc, bk=acc[par], nb=nb, hp=hp, c0=c0: e.tensor_tensor(
                            out=hm[nb:nb + 64, 4 + hp, c0:c0 + 512], in0=bank(bk)[nb:nb + 64, :],
                            in1=rc[nb:nb + 64, :], op=ALU.mult),
                            reads=[R_bank[acc[par]]] + r_rc, writes=[R_hm[4 + hp][c0 // 128 + t] for t in range(4)])

            emit_st(0)
            for i in range(len(steps)):
                if i + 1 < len(steps):
                    emit_st(i + 1)
                emit_rest(i)
            for T in range(NT):
                for dh in range(2):
                    bk = next_bank([0, 1, 2, 3])
                    for c in range(4):
                        P.emit('pe', lambda e, bk=bk, c=c, T=T, dh=dh: e.matmul(
                            bank(bk), lhsT=hm[:, 4 + c, T * 128:(T + 1) * 128], rhs=wout_a[:, c, dh * 512:(dh + 1) * 512],
                            start=(c == 0), stop=(c == 3)), reads=[R_hm[4 + c][T]] + r_wout_a, writes=[R_bank[bk]])
                    x_add(bk, T, dh)

        def dump_x(s):
            for T in range(NT):
                P.emit('sp', lambda e, T=T, s=s: e.dma_start(
                    out=y_d[s * S + T * 128:s * S + (T + 1) * 128, :], in_=x[:, T, :]), reads=R_x[T], dma=True, is_out=True)

        for s in range(NSEQ if stop is None else 1):
            g1 = load_gain("ffn1_norm")
            for T in range(NT):
                P.emit('sp', lambda e, T=T, s=s: e.dma_start(out=x[:, T, :], in_=x_d[s * S + T * 128:s * S + (T + 1) * 128, :]),
                       writes=R_x[T], dma=True)
            if s == 0:
                load_group(1, 0)
            if stop == 'w0':
                wg, r0, wu, r1, wd, r2 = wbuf_views(0)
                P.emit('pool', lambda e: e.dma_start(out=y_d[0:512, :].rearrange("(p r) f -> p (r f)", r=4),
                                                     in_=wg.rearrange("p c f -> p (c f)")), reads=r0, dma=True, is_out=True)
                P.emit('pool', lambda e: e.dma_start(out=y_d[512:1024, :].rearrange("(p r) f -> p (r f)", r=4),
                                                     in_=wd.rearrange("p c f -> p (c f)")), reads=r2, dma=True, is_out=True)
                break
            norm_to_hT(g1)
            ffn(1, pre_loaded=True)
            if stop == 'w4':
                wg, r0, wu, r1, wd, r2 = wbuf_views(0)
                P.emit('pool', lambda e: e.dma_start(out=y_d[0:512, :].rearrange("(p r) f -> p (r f)", r=4),
                                                     in_=wg.rearrange("p c f -> p (c f)")), reads=r0, dma=True, is_out=True)
                P.emit('pool', lambda e: e.dma_start(out=y_d[512:1024, :].rearrange("(p r) f -> p (r f)", r=4),
                                                     in_=wd.rearrange("p c f -> p (c f)")), reads=r2, dma=True, is_out=True)
                break
            if stop == 'ffn1':
                dump_x(s); break
            if 'skipmix' not in DBG:
                gm = load_gain("mix_norm")
                norm_to_hT(gm)
                mix_p1_p2()
                if stop == 'p2':
                    dump_x(s); break
                mix_p3()
                if 'nopre' not in DBG:
                    load_group(2, 0)
                mix_p4()
                if stop == 'mix':
                    dump_x(s); break
                if 'nopre' in DBG:
                    load_group(2, 0)
            else:
                load_group(2, 0)
            g2 = load_gain("ffn2_norm")
            norm_to_hT(g2)
            ffn(2, pre_loaded=True)
            if stop == 'ffn2':
                dump_x(s); break
            gf = load_gain("final_norm")
            if s + 1 < NSEQ:
                load_group(1, 0)
            rms_stats()
            for T in range(NT):
                ys, r_ys = ystage[T % 2]
                P.emit('dve', lambda e, T=T, ys=ys, gf=gf: e.scalar_tensor_tensor(
                    out=ys[:], in0=x[:, T, :], scalar=rstd[:, T:T + 1], in1=gbuf[gf][:], op0=ALU.mult, op1=ALU.mult),
                    reads=R_x[T] + [R_rstd, R_g[gf]], writes=r_ys)
                P.emit('sp', lambda e, T=T, s=s, ys=ys: e.dma_start(
                    out=y_d[s * S + T * 128:s * S + (T + 1) * 128, :], in_=ys[:]), reads=r_ys, dma=True, is_out=True)
        P.finish('sp')
        P.replay(st)
    return nc


def _constants():
    bf = ml_dtypes.bfloat16
    c = {}
    c["ident"] = np.eye(128, dtype=np.float32).astype(bf)
    k = np.arange(128)
    ang = 2.0 * np.pi * ((k[:, None] * k[None, :]) % 128) / 128.0
    c["ccsc"] = np.concatenate([np.cos(ang), np.sin(ang)], axis=1).astype(np.float32) / np.float32(math.sqrt(128.0))
    c["ccsc"] = c["ccsc"].astype(bf)
    n = np.arange(S)
    angs = 2.0 * np.pi * ((n[:, None] * n[None, :]) % S) / float(S)
    Cs = (np.cos(angs) / math.sqrt(S)).astype(np.float32)
    Ss = (-np.sin(angs) / math.sqrt(S)).astype(np.float32)
    cs = np.stack([Cs.reshape(16, 128, S), Ss.reshape(16, 128, S)], axis=2)
    c["csmat"] = np.ascontiguousarray(cs).astype(bf)
    t = np.arange(S)
    row = (t // 64).astype(np.float64)
    col = (t % 64).astype(np.float64)
    inv = 10000.0 ** (-np.arange(0, 32, 2, dtype=np.float64) / 32.0)
    ar = row[:, None] * inv[None, :]
    ac = col[:, None] * inv[None, :]
    cos64 = np.concatenate([np.cos(ar), np.cos(ar), np.cos(ac), np.cos(ac)], axis=1)
    sin64 = np.concatenate([-np.sin(ar), np.sin(ar), -np.sin(ac), np.sin(ac)], axis=1)
    c["cos_t"] = np.ascontiguousarray(cos64.reshape(NT, 128, 64).transpose(1, 0, 2).reshape(128, NT * 64)).astype(np.float32)
    c["sin_t"] = np.ascontiguousarray(sin64.reshape(NT, 128, 64).transpose(1, 0, 2).reshape(128, NT * 64)).astype(np.float32)
    return c


_CACHE = {}


def kernel(x, ffn1_norm, ffn1_w_gate, ffn1_w_up, ffn1_w_down, mix_norm, w_in, fnet_w, fnet_b, q_norm, k_norm,
           w_out, ffn2_norm, ffn2_w_gate, ffn2_w_up, ffn2_w_down, final_norm):
    f32 = lambda a: np.ascontiguousarray(np.asarray(a, dtype=np.float32))
    x = f32(x)
    if "nc" not in _CACHE:
        _CACHE["nc"] = build_program()
        _CACHE["const"] = _constants()
    nc = _CACHE["nc"]
    const = _CACHE["const"]
    shared = {
        "ffn1_norm": f32(ffn1_norm).reshape(1, D), "mix_norm": f32(mix_norm).reshape(1, D),
        "ffn2_norm": f32(ffn2_norm).reshape(1, D), "final_norm": f32(final_norm).reshape(1, D),
        "ffn1_w_gate": f32(ffn1_w_gate), "ffn1_w_up": f32(ffn1_w_up), "ffn1_w_down": f32(ffn1_w_down),
        "ffn2_w_gate": f32(ffn2_w_gate), "ffn2_w_up": f32(ffn2_w_up), "ffn2_w_down": f32(ffn2_w_down),
        "w_in": f32(w_in), "w_out": f32(w_out), "fnet_w": f32(fnet_w),
        "fnet_bT": np.ascontiguousarray(f32(fnet_b).T),
        "gqk": np.concatenate([np.tile(f32(q_norm), 8), np.tile(f32(k_norm), 2)]).reshape(1, 640),
    }
    shared.update(const)
    xs = x.reshape(N_CORES, NSEQ * S, D)
    in_maps = []
    for c in range(N_CORES):
        m = dict(shared)
        m["x"] = np.ascontiguousarray(xs[c])
        in_maps.append(m)
    res = run_bass_kernel_spmd(nc, in_maps, core_ids=list(range(N_CORES)))
    out = np.stack([np.asarray(r["y"], dtype=np.float32) for r in res.results], axis=0)
    return out.reshape(16, S, D)
```
